# Optimizing a Trainium2 kernel written in Bass

```python
import jax, jax.numpy as jnp
from jax import lax
import numpy as np

D_MODEL = 1024
BATCH = 4
SEQ = 8192
DEPTH = 1

CONV_W = D_MODEL // 2
N_HEADS = 8
HEAD_DIM = (D_MODEL - CONV_W) // N_HEADS
ATTN_W = N_HEADS * HEAD_DIM
IN_COLS = 3 * CONV_W + 3 * ATTN_W
CONV_K = 3
D_FF = 2816
PLE_DIM = 256
DILATED_PAIRS = ((128, 1), (512, 4), (2048, 16))
BLOCK = 128
EPS = 1e-6

kernel_name = 'hybrid_conv_dilated_attn_convffn_ple'


def rmsnorm(a, g):
    af = a.astype(jnp.float32)
    af = af * lax.rsqrt(jnp.mean(af * af, axis=-1, keepdims=True) + EPS)
    return (af * g.astype(jnp.float32)).astype(a.dtype)


def causal_dwconv3(u, w, b):
    up = jnp.pad(u, ((0, 0), (CONV_K - 1, 0), (0, 0)))
    t = u.shape[1]
    return up[:, 0:t] * w[0] + up[:, 1:t + 1] * w[1] + up[:, 2:t + 2] * w[2] + b


def alibi_slopes(n):
    return jnp.exp2(-8.0 * jnp.arange(1, n + 1, dtype=jnp.float32) / n)


def dilated_branch(q, k, v, slopes, window, dilation):
    b, t, h, hd = q.shape
    steps = window // dilation
    L = t // dilation
    nb = -(-L // BLOCK)
    lp = nb * BLOCK

    def to_blocks(a):
        a = a.reshape(b, L, dilation, h, hd).transpose(0, 2, 3, 1, 4)
        a = jnp.pad(a, ((0, 0), (0, 0), (0, 0), (0, lp - L), (0, 0)))
        return a.reshape(b, dilation, h, nb, BLOCK, hd)

    def with_prev(a):
        prev = jnp.pad(a[:, :, :, :-1], ((0, 0), (0, 0), (0, 0), (1, 0), (0, 0), (0, 0)))
        return jnp.concatenate([prev, a], axis=4)

    qb = to_blocks(q)
    kk = with_prev(to_blocks(k))
    vv = with_prev(to_blocks(v))

    s = jnp.einsum('brhnqd,brhnkd->brhnqk', qb, kk) * (hd ** -0.5)
    qi = jnp.arange(BLOCK)[:, None] + BLOCK
    kj = jnp.arange(2 * BLOCK)[None, :]
    step = qi - kj
    band = (step >= 0) & (step <= steps)
    first = (jnp.arange(nb)[:, None, None] == 0) & (kj < BLOCK)[None]
    mask = band[None] & ~first
    dist = (step * dilation).astype(jnp.float32)
    bias = -slopes[:, None, None] * dist[None]
    s = jnp.where(mask, s + bias[:, None], -jnp.inf)
    m = jnp.max(s, axis=-1, keepdims=True)
    e = jnp.exp(s - m)
    den = jnp.sum(e, axis=-1, keepdims=True)
    o = jnp.einsum('brhnqk,brhnkd->brhnqd', e, vv) / den
    lse = (m + jnp.log(den))[..., 0]

    o = o.reshape(b, dilation, h, lp, hd)[:, :, :, :L]
    o = o.transpose(0, 3, 1, 2, 4).reshape(b, t, h, hd)
    lse = lse.reshape(b, dilation, h, lp)[:, :, :, :L]
    lse = lse.transpose(0, 3, 1, 2).reshape(b, t, h)
    return o, lse


def dilated_attention(q, k, v):
    slopes = alibi_slopes(q.shape[2])
    outs, lses = [], []
    for window, dilation in DILATED_PAIRS:
        o, lse = dilated_branch(q, k, v, slopes, window, dilation)
        outs.append(o)
        lses.append(lse)
    wts = jax.nn.softmax(jnp.stack(lses, axis=0), axis=0)
    o = jnp.sum(wts[..., None] * jnp.stack(outs, axis=0), axis=0)
    return o


def setup_inputs(seed: int = 0) -> dict:
    key = jax.random.key(seed)
    ks = jax.random.split(key, 24)
    f32 = jnp.float32

    def nrm(k, shape, fan):
        return jax.random.normal(k, shape, f32) * (fan ** -0.5)

    def gain(k, shape):
        return 1.0 + 0.05 * jax.random.normal(k, shape, f32)

    def bias(k, shape):
        return 0.01 * jax.random.normal(k, shape, f32)

    return {
        'x': jax.random.normal(ks[0], (BATCH, SEQ, D_MODEL), f32),
        'p': jax.random.normal(ks[1], (DEPTH, BATCH, SEQ, PLE_DIM), f32),
        'g_mix': gain(ks[2], (DEPTH, D_MODEL)),
        'w_in': nrm(ks[3], (DEPTH, D_MODEL, IN_COLS), D_MODEL),
        'conv_w': nrm(ks[4], (DEPTH, CONV_K, CONV_W), CONV_K),
        'conv_b': bias(ks[5], (DEPTH, CONV_W)),
        'q_norm_g': gain(ks[6], (DEPTH, HEAD_DIM)),
        'k_norm_g': gain(ks[7], (DEPTH, HEAD_DIM)),
        'g_out_conv': gain(ks[8], (DEPTH, CONV_W)),
        'g_out_attn': gain(ks[9], (DEPTH, ATTN_W)),
        'w_out': nrm(ks[10], (DEPTH, CONV_W + ATTN_W, D_MODEL), CONV_W + ATTN_W),
        'g_ffn': gain(ks[11], (DEPTH, D_MODEL)),
        'w_gate': nrm(ks[12], (DEPTH, D_MODEL, D_FF), D_MODEL),
        'w_up': nrm(ks[13], (DEPTH, D_MODEL, D_FF), D_MODEL),
        'ffn_conv_w': nrm(ks[14], (DEPTH, CONV_K, D_FF), CONV_K),
        'ffn_conv_b': bias(ks[15], (DEPTH, D_FF)),
        'w_down': nrm(ks[16], (DEPTH, D_FF, D_MODEL), D_FF),
        'g_ple': gain(ks[17], (DEPTH, D_MODEL)),
        'w_ple_gate': nrm(ks[18], (DEPTH, D_MODEL, D_MODEL), D_MODEL),
        'w_ple_proj': nrm(ks[19], (DEPTH, PLE_DIM, D_MODEL), PLE_DIM),
    }


def reference(x, p, g_mix, w_in, conv_w, conv_b, q_norm_g, k_norm_g, g_out_conv,
              g_out_attn, w_out, g_ffn, w_gate, w_up, ffn_conv_w, ffn_conv_b, w_down,
              g_ple, w_ple_gate, w_ple_proj):
    b, t, _ = x.shape
    for i in range(DEPTH):
        h = rmsnorm(x, g_mix[i])
        z = h @ w_in[i]
        zb, zc, zx, zq, zk, zv = jnp.split(
            z, np.cumsum([CONV_W, CONV_W, CONV_W, ATTN_W, ATTN_W]), axis=-1)
        y_c = zb * causal_dwconv3(zc * zx, conv_w[i], conv_b[i])
        q = rmsnorm(zq.reshape(b, t, N_HEADS, HEAD_DIM), q_norm_g[i]).astype(jnp.float32)
        k = rmsnorm(zk.reshape(b, t, N_HEADS, HEAD_DIM), k_norm_g[i]).astype(jnp.float32)
        v = zv.reshape(b, t, N_HEADS, HEAD_DIM).astype(jnp.float32)
        y_a = dilated_attention(q, k, v).reshape(b, t, ATTN_W).astype(x.dtype)
        y = jnp.concatenate([rmsnorm(y_c, g_out_conv[i]), rmsnorm(y_a, g_out_attn[i])], axis=-1)
        x = x + y @ w_out[i]
        h = rmsnorm(x, g_ffn[i])
        gate = causal_dwconv3(h @ w_gate[i], ffn_conv_w[i], ffn_conv_b[i])
        x = x + (jax.nn.silu(gate) * (h @ w_up[i])) @ w_down[i]
        ple_gate = jax.nn.sigmoid(rmsnorm(x, g_ple[i]) @ w_ple_gate[i])
        x = x + ple_gate * (p[i].astype(x.dtype) @ w_ple_proj[i])
    return x
```

```python
import math
from contextlib import ExitStack

import numpy as np
import concourse.bass as bass
import concourse.mybir as mybir
from concourse.bass_utils import run_bass_kernel_spmd

F32 = mybir.dt.float32
BF16 = mybir.dt.bfloat16
AF = mybir.ActivationFunctionType
ALU = mybir.AluOpType

D_MODEL = 1024
SEQ = 8192
BATCH = 4
D_FF = 2816
PLE_DIM = 256
EPS = 1e-6
TW = 512
OWN = 4096
FIRST_TILE = 1
N_KV = 4
HALO_TILE = 5
LAST_TILE = 13
NTOK = (LAST_TILE - FIRST_TILE + 1) * TW
NEGV = -240000.0

C_ID, C_BD, C_O1024, C_O512, C_MASK0 = 0, 1, 2, 3, 4
N_CBLK = 4 + 24
P_GMIX, P_CW, P_CB, P_QG, P_KG, P_GOC, P_GOA, P_GFFN, P_FCW, P_FCB, P_GPLE = 0, 8, 20, 24, 25, 26, 30, 34, 42, 108, 130
NPRM = 138

FF_SLABS = [(0, 4), (4, 4), (8, 3), (11, 4), (15, 4), (19, 3)]


class Prog:
    def __init__(self, esem):
        self.q = {k: [] for k in ("pe", "act", "dve", "pool", "sp")}
        self.cnt = {k: 0 for k in ("pe", "act", "dve", "pool")}
        self.esem = esem
        self.res = {}
        self.seen = {}

    def _wait(self, waiter, stamp, is_raw):
        sem, val, owner, sname = stamp
        if owner == waiter:
            if waiter == "pe" or not is_raw:
                return
        key = (waiter, sname)
        if self.seen.get(key, 0) >= val:
            return
        self.seen[key] = val
        self.q[waiter].append(lambda e, sem=sem, val=val: e.wait_ge(sem, val))

    def _deps(self, waiter, reads, writes):
        for r in reads:
            w = self.res.get(r)
            if w and w[0]:
                self._wait(waiter, w[0], True)
        for wn in writes:
            w = self.res.get(wn)
            if w:
                if w[0]:
                    self._wait(waiter, w[0], False)
                for st in w[1].values():
                    self._wait(waiter, st, False)

    def _update(self, key, stamp, reads, writes):
        for wn in writes:
            self.res[wn] = [stamp, {}]
        for r in reads:
            self.res.setdefault(r, [None, {}])[1][key] = stamp

    def op(self, eng, fn, reads=(), writes=(), inc=True):
        self._deps(eng, reads, writes)
        sem = self.esem[eng]
        if inc:
            self.cnt[eng] += 1
            stamp = (sem, self.cnt[eng], eng, eng)
            self.q[eng].append(lambda e, fn=fn, sem=sem: fn(e).then_inc(sem, 1))
        else:
            assert eng == "pe"
            stamp = (sem, self.cnt[eng] + 1, eng, eng)
            self.q[eng].append(lambda e, fn=fn: fn(e))
        self._update(eng, stamp, reads, writes)

    def dma(self, queue, out, in_, reads, writes, sem, sname, target):
        self._deps(queue, reads, writes)
        self.q[queue].append(lambda e, out=out, in_=in_, sem=sem: e.dma_start(out=out, in_=in_).then_inc(sem, 16))
        stamp = (sem, target, None, sname)
        self._update(sname, stamp, reads, writes)

    def dma_batch(self, queue, pairs, reads, writes, sem, sname, target):
        self._deps(queue, reads, writes)
        for out, in_ in pairs:
            self.q[queue].append(lambda e, out=out, in_=in_, sem=sem: e.dma_start(out=out, in_=in_).then_inc(sem, 16))
        stamp = (sem, target, None, sname)
        self._update(sname, stamp, reads, writes)

    def wait_final(self, waiter, sem, val):
        self.q[waiter].append(lambda e: e.wait_ge(sem, val))


def build_program(last_tile=LAST_TILE):
    nc = bass.Bass("TRN2", target_bir_lowering=False)
    dt = nc.dram_tensor
    xin = dt("xT", [D_MODEL, NTOK], F32, kind="ExternalInput").ap().rearrange("(c p) t -> p c t", p=128)
    pin = dt("pT", [PLE_DIM, OWN], F32, kind="ExternalInput").ap().rearrange("(c p) t -> p c t", p=128)
    vmin = dt("vmask", [128, NTOK // 128], F32, kind="ExternalInput").ap()
    prmin = dt("prm", [128, NPRM], F32, kind="ExternalInput").ap()
    cstin = dt("cst", [128, N_CBLK * 128], F32, kind="ExternalInput").ap()
    w_in = dt("w_in", [D_MODEL, 3072], F32, kind="ExternalInput").ap().rearrange("(c p) n -> p c n", p=128)
    w_out = dt("w_out", [D_MODEL, D_MODEL], F32, kind="ExternalInput").ap().rearrange("(c p) n -> p c n", p=128)
    w_gate = dt("w_gate", [D_MODEL, D_FF], F32, kind="ExternalInput").ap().rearrange("(c p) n -> p c n", p=128)
    w_up = dt("w_up", [D_MODEL, D_FF], F32, kind="ExternalInput").ap().rearrange("(c p) n -> p c n", p=128)
    w_down = dt("w_down", [D_FF, D_MODEL], F32, kind="ExternalInput").ap().rearrange("(q j p) n -> q p j n", q=2, j=11, p=128)
    w_pg = dt("w_pg", [D_MODEL, D_MODEL], F32, kind="ExternalInput").ap().rearrange("(c p) n -> p c n", p=128)
    w_pp = dt("w_pp", [PLE_DIM, D_MODEL], F32, kind="ExternalInput").ap().rearrange("(c p) n -> p c n", p=128)
    outT = dt("outT", [D_MODEL, OWN], F32, kind="ExternalOutput").ap().rearrange("(c p) t -> p c t", p=128)
    s_win = dt("s_win", [6, 128, 8, 512], BF16, kind="Internal").ap()
    s_wout = dt("s_wout", [2, 128, 8, 512], BF16, kind="Internal").ap()
    s_wg = dt("s_wg", [6, 128, 8, 512], BF16, kind="Internal").ap()
    s_wu = dt("s_wu", [6, 128, 8, 512], BF16, kind="Internal").ap()
    s_wd = dt("s_wd", [8, 128, 11, 256], BF16, kind="Internal").ap()
    s_wpg = dt("s_wpg", [2, 128, 8, 512], BF16, kind="Internal").ap()
    s_wpp = dt("s_wpp", [128, 2, 1024], BF16, kind="Internal").ap()
    s_pbf = dt("s_pbf", [128, 2, OWN], BF16, kind="Internal").ap()

    es = ExitStack()
    with es:
        def sb(name, shape, dtype):
            return es.enter_context(nc.sbuf_tensor(name, shape, dtype))

        def sem(name):
            return es.enter_context(nc.semaphore(name))

        KT = sb("KT", [128, 4, 4096], BF16)
        V1 = sb("V1", [128, 8, 520], BF16)
        V4 = sb("V4", [128, 8, 520], BF16)
        V16 = sb("V16", [128, 32, 520], BF16)
        xT = sb("xTs", [128, 8, TW], F32)
        hT = sb("hT", [128, 8, TW], BF16)
        QT = sb("QT", [128, 4, TW], BF16)
        yT = sb("yT", [128, 8, TW], BF16)
        yf = sb("yf", [128, 4, TW], F32)
        gT = sb("gT", [128, 11, TW], BF16)
        wr = sb("wring", [128, 3, 4096], BF16)
        cbf = sb("cbf", [128, N_CBLK * 128], BF16)
        prm = sb("prms", [128, NPRM], F32)
        vm = sb("vms", [128, NTOK // 128], F32)
        onesf = sb("onesf", [128, 64], F32)
        sqb = [sb(f"sqb{i}", [128, TW], BF16) for i in range(2)]
        rstd = sb("rstd", [128, TW], F32)
        rq = [sb(f"rq{i}", [128, TW], F32) for i in range(2)]
        ub = [sb(f"ub{i}", [128, TW + 2], F32) for i in range(2)]
        acc = [sb(f"acc{i}", [128, TW], F32) for i in range(2)]
        xj = sb("xj", [128, TW], F32)
        osb = [sb(f"osb{i}", [128, TW], F32) for i in range(2)]
        rden = sb("rden", [128, TW], F32)
        PT = [sb(f"PT{i}", [128, TW], BF16) for i in range(3)]
        sig = [sb(f"sig{i}", [128, TW], F32) for i in range(2)]
        pTs = sb("pTs", [128, 2, TW], BF16)
        ucar = sb("ucar", [128, 4, 2], F32)
        gcar = sb("gcar", [128, 22, 2], F32)
        ps = [es.enter_context(nc.psum_tensor(f"ps{i}", [128, 512], F32)) for i in range(8)]

        esem = {k: sem("e_" + k) for k in ("pe", "act", "dve", "pool")}
        P = Prog(esem)
        s_x, s_p, s_o, s_c = sem("s_x"), sem("s_p"), sem("s_o"), sem("s_c")
        s_w = [sem(f"s_w{i}") for i in range(3)]
        s_vs = [sem(f"s_vs{i}") for i in range(2)]
        cnt = {"x": 0, "p": 0, "o": 0, "c": 0, "w": [0, 0, 0], "vs": [0, 0]}

        def cblk(i, n=1):
            return cbf[:, i * 128:(i + n) * 128]

        def pcol(i):
            return prm[:, i:i + 1]

        s_cb = sem("s_cb")
        P.dma("pool", cbf[:], cstin[:, :], [], ["cbf"], s_cb, "s_cb", 16)
        P.dma_batch("sp", [(prm[:], prmin[:, :]), (vm[:], vmin[:, :])], [], ["prm", "vm"], s_c, "s_c", 32)
        P.op("dve", lambda e: e.memset(onesf[:], 1.0), [], ["onesf"])
        for c in range(4):
            P.op("pool", lambda e, c=c: e.memset(KT[:, c, :], 0.0), [], [f"KT{c}_{b}" for b in range(8)])
        P.op("pool", lambda e: e.memset(V1[:], 0.0), [], [f"V1_{i}" for i in range(8)])
        P.op("pool", lambda e: e.memset(V4[:], 0.0), [], [f"V4_{i}" for i in range(8)])
        P.op("pool", lambda e: e.memset(V16[:], 0.0), [], [f"V16_{i}" for i in range(32)])
        P.op("pool", lambda e: e.memset(ucar[:], 0.0), [], ["ucar"])
        P.op("pool", lambda e: e.memset(gcar[:], 0.0), [], ["gcar"])

        def cast_group(name, dst, src):
            s = sem("c_" + name)
            P.dma("pool", dst, src, [], [name], s, name, 16)

        cast_group("win4", s_win[4], w_in[:, :, 2048:2560])
        cast_group("win5", s_win[5], w_in[:, :, 2560:3072])
        for j in range(4):
            for t in range(4):
                cast_group(f"win{j}_{t}", s_win[j, :, :, 128 * t:128 * t + 128],
                           w_in[:, :, 512 * t + 128 * j:512 * t + 128 * j + 128])
        for s in range(2):
            cast_group(f"wout{s}", s_wout[s], w_out[:, :, 512 * s:512 * s + 512])
        for i, (j0, n) in enumerate(FF_SLABS):
            cast_group(f"wg{i}", s_wg[i, :, :, 0:128 * n], w_gate[:, :, 128 * j0:128 * (j0 + n)])
            cast_group(f"wu{i}", s_wu[i, :, :, 0:128 * n], w_up[:, :, 128 * j0:128 * (j0 + n)])
            if i % 3 == 2:
                q = i // 3
                for o2 in range(4):
                    cast_group(f"wd{q * 4 + o2}", s_wd[q * 4 + o2], w_down[q][:, :, 256 * o2:256 * o2 + 256])
        cast_group("wpp", s_wpp[:, :, :], w_pp[:, :, :])
        for s in range(2):
            cast_group(f"wpg{s}", s_wpg[s], w_pg[:, :, 512 * s:512 * s + 512])
        cast_group("pbf", s_pbf[:, :, :], pin[:, :, :])

        wstate = {"i": 0}

        def load_slab(scr_name, src_ap, shape):
            slot = wstate["i"] % 3
            wstate["i"] += 1
            n = 1
            for d in shape[1:]:
                n *= d
            dst = wr[:, slot, 0:n]
            if len(shape) == 3:
                dst = dst.rearrange("p (a b) -> p a b", b=shape[2])
            cnt["w"][slot] += 16
            P.dma("sp", dst, src_ap, [scr_name] if isinstance(scr_name, str) else list(scr_name), [f"w{slot}"], s_w[slot],
                  f"s_w{slot}", cnt["w"][slot])
            return dst, f"w{slot}"

        mmi = {"i": 0}

        def mm_bank():
            b = mmi["i"] % 4
            mmi["i"] += 1
            return b

        def proj(bank, wv, wname, rhs_fn, rhs_res, ocol, nk=8, extra_reads=()):
            for c in range(nk):
                P.op("pe", lambda e, c=c: e.matmul(ps[bank][:], wv[:, c, ocol:ocol + 128], rhs_fn(c),
                                                   start=(c == 0), stop=(c == nk - 1)),
                     [wname] + rhs_res(c) + list(extra_reads), [f"ps{bank}"], inc=(c == nk - 1))

        def sqrt_recip(dst, dname, bank):
            P.op("act", lambda e: e.activation(out=dst[:], in_=ps[bank][:], func=AF.Sqrt, bias=EPS, scale=1.0),
                 [f"ps{bank}"], [dname])
            P.op("dve", lambda e: e.reciprocal(out=dst[:], in_=dst[:]), [dname], [dname])

        def rmsnorm_x(gcol0):
            bank = mm_bank()
            for c in range(8):
                sq = sqb[c % 2]
                P.op("act", lambda e, c=c, sq=sq: e.activation(out=sq[:], in_=xT[:, c, :], func=AF.Square),
                     [f"xT{c}"], [f"sqb{c % 2}"])
                P.op("pe", lambda e, c=c, sq=sq: e.matmul(ps[bank][:], cblk(C_O1024), sq[:], start=(c == 0), stop=(c == 7)),
                     [f"sqb{c % 2}", "cbf"], [f"ps{bank}"], inc=True)
            sqrt_recip(rstd, "rstd", bank)
            for c in range(8):
                P.op("dve", lambda e, c=c: e.scalar_tensor_tensor(out=hT[:, c, :], in0=xT[:, c, :], scalar=pcol(gcol0 + c),
                                                                  in1=rstd[:], op0=ALU.mult, op1=ALU.mult),
                     [f"xT{c}", "rstd", "prm"], [f"hT{c}"])

        qki = {"i": 0}

        def qk_norm(bank, dst_ap, dst_res, gcol):
            i = qki["i"] % 2
            qki["i"] += 1
            sq = sqb[i]
            P.op("act", lambda e: e.activation(out=sq[:], in_=ps[bank][:], func=AF.Square), [f"ps{bank}"], [f"sqb{i}"])
            nb = mm_bank()
            P.op("pe", lambda e: e.matmul(ps[nb][:], cblk(C_BD), sq[:], start=True, stop=True), [f"sqb{i}", "cbf"], [f"ps{nb}"])
            sqrt_recip(rq[i], f"rq{i}", nb)
            P.op("dve", lambda e: e.scalar_tensor_tensor(out=dst_ap, in0=ps[bank][:], scalar=pcol(gcol), in1=rq[i][:],
                                                         op0=ALU.mult, op1=ALU.mult),
                 [f"ps{bank}", f"rq{i}", "prm"], dst_res)

        hT_res = lambda c: [f"hT{c}"]

        def kt_res(ch, k):
            return [f"KT{ch}_{k % 8}"]

        def k_and_v(k):
            u0 = TW * k
            rc = u0 % 4096
            wv, wn = load_slab("win4", s_win[4], [128, 8, 512])
            for ch in range(4):
                bank = mm_bank()
                proj(bank, wv, wn, lambda c: hT[:, c, :], hT_res, 128 * ch)
                qk_norm(bank, KT[:, ch, rc:rc + TW], kt_res(ch, k), P_KG)
            wv, wn = load_slab("win5", s_win[5], [128, 8, 512])
            T4, T16, kq = k, k // 4, k % 4
            par = k % 2
            pairs = []
            for n in range(4):
                tn = 4 * k + n
                slot = tn % 8
                bank = mm_bank()
                for c in range(8):
                    P.op("pe", lambda e, c=c, n=n, bank=bank, wv=wv: e.matmul(ps[bank][:], hT[:, c, 128 * n:128 * n + 128], wv[:, c, :],
                                                            start=(c == 0), stop=(c == 7)),
                         [wn, f"hT{c}"], [f"ps{bank}"], inc=(c == 7))
                v1v = V1[:, slot, :].rearrange("p (h e) -> p h e", e=65)
                P.op("act", lambda e, v1v=v1v, bank=bank: e.activation(out=v1v[:, :, 0:64],
                                                                       in_=ps[bank][:].rearrange("p (h d) -> p h d", d=64),
                                                                       func=AF.Copy),
                     [f"ps{bank}"], [f"V1_{slot}"])
                vcol = tn - 4 * FIRST_TILE
                P.op("pool", lambda e, v1v=v1v, vcol=vcol: e.tensor_copy(out=v1v[:, :, 64:65],
                                                                         in_=vm[:, vcol:vcol + 1].unsqueeze(1).broadcast_to([128, 8, 1])),
                     ["vm"], [f"V1_{slot}"])
                for r in range(4):
                    pairs.append((V4[32 * n:32 * n + 32, r * 2 + T4 % 2, :], V1[r:128:4, slot, :]))
                for r in range(16):
                    p0 = 32 * kq + 8 * n
                    pairs.append((V16[p0:p0 + 8, r * 2 + T16 % 2, :], V1[r:128:16, slot, :]))
            cnt["vs"][par] += 16 * len(pairs)
            P.dma_batch("sp", pairs, [f"V1_{(4 * k + n) % 8}" for n in range(4)],
                        [f"V4_{r * 2 + T4 % 2}" for r in range(4)] + [f"V16_{r * 2 + T16 % 2}" for r in range(16)],
                        s_vs[par], f"s_vs{par}", cnt["vs"][par])

        def load_x(k):
            c0 = TW * (k - FIRST_TILE)
            cnt["x"] += 16
            P.dma("sp", xT[:], xin[:, :, c0:c0 + TW], [], [f"xT{c}" for c in range(8)], s_x, "s_x", cnt["x"])

        sti = {"i": 0}
        pti = {"i": 0}

        def attention(k):
            u0 = TW * k
            T16, kq = k // 4, k % 4
            o16 = 32 * kq
            for h in range(8):
                ch, pb = h // 2, 64 * (h % 2)
                ob = 6 + h % 2
                first_pv = [True]
                branches = []
                for d in (1, 4, 16):
                    ci = 2 - h + int(math.log2(d)) + 5
                    mbase = (C_MASK0 + 2 * ci) * 128
                    for half in range(2):
                        blocks = []
                        if d == 1:
                            mrhs = cbf[:, mbase:mbase + 256].unsqueeze(1).broadcast_to([128, 2, 256])
                            for qb in (2 * half, 2 * half + 1):
                                T1 = 4 * k + qb
                                ents = []
                                for T in (T1 - 1, T1):
                                    col = (128 * T) % 4096
                                    ents.append((KT[pb:pb + 64, ch, col:col + 128], kt_res(ch, T // 4),
                                                 V1[:, T % 8, 65 * h:65 * h + 65], f"V1_{T % 8}"))
                                blocks.append((QT[pb:pb + 64, ch, 128 * qb:128 * qb + 128],
                                               ps[ob][0:65, 128 * qb:128 * qb + 128], ents))
                            bw = 128
                        elif d == 4:
                            mrhs = cbf[:, mbase:mbase + 256].unsqueeze(1).broadcast_to([128, 2, 256])
                            for r in (2 * half, 2 * half + 1):
                                ents = []
                                for T in (k - 1, k):
                                    base = (TW * T) % 4096
                                    ents.append((KT[pb:pb + 64, ch, base + r:base + 512:4], kt_res(ch, T),
                                                 V4[:, r * 2 + T % 2, 65 * h:65 * h + 65], f"V4_{r * 2 + T % 2}"))
                                blocks.append((QT[pb:pb + 64, ch, r:512:4], ps[ob][0:65, r:512:4], ents))
                            bw = 128
                        else:
                            mrhs = cbf[:, mbase:mbase + 256].rearrange("p (t j) -> p t j", j=128)[:, :, o16:o16 + 32] \
                                .unsqueeze(1).broadcast_to([128, 8, 2, 32])
                            for r in range(8 * half, 8 * half + 8):
                                ents = []
                                for T in (T16 - 1, T16):
                                    base = (2048 * T) % 4096
                                    kres = [f"KT{ch}_{(4 * T + i) % 8}" for i in range(4)]
                                    ents.append((KT[pb:pb + 64, ch, base + r:base + 2048:16], kres,
                                                 V16[:, r * 2 + T % 2, 65 * h:65 * h + 65], f"V16_{r * 2 + T % 2}"))
                                blocks.append((QT[pb:pb + 64, ch, r:512:16], ps[ob][0:65, r:512:16], ents))
                            bw = 32
                        branches.append((mrhs, blocks, bw))
                for (mrhs, blocks, bw) in branches:
                    sbk = 4 + sti["i"] % 2
                    sti["i"] += 1
                    P.op("pe", lambda e, mrhs=mrhs, sbk=sbk: e.matmul(ps[sbk][:], cblk(C_ID), mrhs, start=True, stop=False,
                                                                      skip_group_check=True),
                         ["cbf"], [f"ps{sbk}"], inc=False)
                    nmm = 2 * len(blocks)
                    i = 0
                    for bi, (qap, oap, ents) in enumerate(blocks):
                        for ei, (kap, kres, vap, vres) in enumerate(ents):
                            c0 = (2 * bi + ei) * bw
                            i += 1
                            P.op("pe", lambda e, kap=kap, qap=qap, c0=c0, sbk=sbk, bw=bw, last=(i == nmm):
                                 e.matmul(ps[sbk][:, c0:c0 + bw], kap, qap, start=False, stop=last, skip_group_check=True),
                                 kres + [f"QT{ch}"], [f"ps{sbk}"], inc=(i == nmm))
                    pt_i = pti["i"] % 3
                    pti["i"] += 1
                    pt = PT[pt_i]
                    P.op("act", lambda e, pt=pt, sbk=sbk: e.activation(out=pt[:], in_=ps[sbk][:], func=AF.Exp, scale=0.125),
                         [f"ps{sbk}"], [f"PT{pt_i}"])
                    i = 0
                    for bi, (qap, oap, ents) in enumerate(blocks):
                        for ei, (kap, kres, vap, vres) in enumerate(ents):
                            c0 = (2 * bi + ei) * bw
                            i += 1
                            st = first_pv[0]
                            first_pv[0] = False
                            P.op("pe", lambda e, oap=oap, vap=vap, pt=pt, c0=c0, bw=bw, st=st:
                                 e.matmul(oap, vap, pt[:, c0:c0 + bw], start=st, stop=False, skip_group_check=True),
                                 [vres, f"PT{pt_i}"], [f"ps{ob}"], inc=(i == nmm))
                ov = osb[h % 2]
                P.op("act", lambda e, ov=ov, ob=ob: e.activation(out=ov[0:65, :], in_=ps[ob][0:65, :], func=AF.Copy),
                     [f"ps{ob}"], [f"osb{h % 2}"])
                P.op("dve", lambda e, ov=ov: e.tensor_scalar_add(out=rden[64:65, :], in0=ov[64:65, :], scalar1=1e-30),
                     [f"osb{h % 2}"], ["rden"])
                P.op("dve", lambda e: e.reciprocal(out=rden[64:65, :], in_=rden[64:65, :]), ["rden"], ["rden"])
                bb = mm_bank()
                P.op("pe", lambda e, bb=bb: e.matmul(ps[bb][0:64, :], onesf[64:65, 0:64], rden[64:65, :], start=True, stop=True),
                     ["rden", "onesf"], [f"ps{bb}"])
                P.op("dve", lambda e, ov=ov, bb=bb, pb=pb, ch=ch: e.tensor_tensor(out=yf[pb:pb + 64, ch, :], in0=ov[0:64, :],
                                                                                  in1=ps[bb][0:64, :], op=ALU.mult),
                     [f"osb{h % 2}", f"ps{bb}"], [f"yf{ch}"])

        def norm_y(lo, gcol0, ones_blk):
            bank = mm_bank()
            for j in range(4):
                sq = sqb[j % 2]
                P.op("act", lambda e, j=j, sq=sq: e.activation(out=sq[:], in_=yf[:, j, :], func=AF.Square),
                     [f"yf{j}"], [f"sqb{j % 2}"])
                P.op("pe", lambda e, j=j, sq=sq: e.matmul(ps[bank][:], cblk(ones_blk), sq[:], start=(j == 0), stop=(j == 3)),
                     [f"sqb{j % 2}", "cbf"], [f"ps{bank}"], inc=True)
            sqrt_recip(rstd, "rstd", bank)
            for j in range(4):
                P.op("dve", lambda e, j=j: e.scalar_tensor_tensor(out=yT[:, lo + j, :], in0=yf[:, j, :], scalar=pcol(gcol0 + j),
                                                                  in1=rstd[:], op0=ALU.mult, op1=ALU.mult),
                     [f"yf{j}", "rstd", "prm"], [f"yT{lo + j}"])

        def mixer_proj(k):
            for j in range(4):
                wv, wn = load_slab([f"win{j}_{t}" for t in range(4)], s_win[j], [128, 8, 512])
                bB, bC, bX, bQ = mm_bank(), mm_bank(), mm_bank(), mm_bank()
                for t, bank in enumerate((bB, bC, bX, bQ)):
                    proj(bank, wv, wn, lambda c: hT[:, c, :], hT_res, 128 * t)
                u = ub[j % 2]
                un = f"ub{j % 2}"
                a = acc[j % 2]
                an = f"acc{j % 2}"
                P.op("act", lambda e, bX=bX: e.activation(out=xj[:], in_=ps[bX][:], func=AF.Copy), [f"ps{bX}"], ["xj"])
                P.op("pool", lambda e, u=u, j=j: e.tensor_copy(out=u[:, 0:2], in_=ucar[:, j, :]), ["ucar"], [un])
                P.op("dve", lambda e, u=u, bC=bC: e.tensor_tensor(out=u[:, 2:TW + 2], in0=ps[bC][:], in1=xj[:], op=ALU.mult),
                     [f"ps{bC}", "xj"], [un])
                P.op("pool", lambda e, u=u, j=j: e.tensor_copy(out=ucar[:, j, :], in_=u[:, TW:TW + 2]), [un], ["ucar"])
                P.op("act", lambda e, u=u, a=a, j=j: e.activation(out=a[:], in_=u[:, 2:TW + 2], func=AF.Identity,
                                                                  bias=pcol(P_CB + j), scale=pcol(P_CW + 8 + j)),
                     [un, "prm"], [an])
                P.op("dve", lambda e, u=u, a=a, j=j: e.scalar_tensor_tensor(out=a[:], in0=u[:, 1:TW + 1], scalar=pcol(P_CW + 4 + j),
                                                                            in1=a[:], op0=ALU.mult, op1=ALU.add),
                     [un, an, "prm"], [an])
                P.op("dve", lambda e, u=u, a=a, j=j: e.scalar_tensor_tensor(out=a[:], in0=u[:, 0:TW], scalar=pcol(P_CW + j),
                                                                            in1=a[:], op0=ALU.mult, op1=ALU.add),
                     [un, an, "prm"], [an])
                P.op("dve", lambda e, a=a, j=j, bB=bB: e.tensor_tensor(out=yf[:, j, :], in0=ps[bB][:], in1=a[:], op=ALU.mult),
                     [f"ps{bB}", an], [f"yf{j}"])
                qk_norm(bQ, QT[:, j, :], [f"QT{j}"], P_QG)
            norm_y(0, P_GOC, C_O512)

        def out_proj():
            for s in range(2):
                wv, wn = load_slab(f"wout{s}", s_wout[s], [128, 8, 512])
                for o in range(4):
                    oc = 4 * s + o
                    bank = mm_bank()
                    proj(bank, wv, wn, lambda c: yT[:, c, :], lambda c: [f"yT{c}"], 128 * o)
                    P.op("dve", lambda e, oc=oc, bank=bank: e.tensor_tensor(out=xT[:, oc, :], in0=xT[:, oc, :], in1=ps[bank][:],
                                                                            op=ALU.add),
                         [f"xT{oc}", f"ps{bank}"], [f"xT{oc}"])

        def ffn(halo):
            for q in range(2):
                for i3 in range(3):
                    i = 3 * q + i3
                    j0, n = FF_SLABS[i]
                    gv, gn = load_slab(f"wg{i}", s_wg[i, :, :, 0:128 * n], [128, 8, 128 * n])
                    if not halo:
                        uv, un_ = load_slab(f"wu{i}", s_wu[i, :, :, 0:128 * n], [128, 8, 128 * n])
                    for t in range(n):
                        j = j0 + t
                        jl = j - 11 * q
                        bg = mm_bank()
                        proj(bg, gv, gn, lambda c: hT[:, c, :], hT_res, 128 * t)
                        if halo:
                            P.op("act", lambda e, j=j, bg=bg: e.activation(out=gcar[:, j, :], in_=ps[bg][:, TW - 2:TW], func=AF.Copy),
                                 [f"ps{bg}"], ["gcar"])
                            continue
                        bu = mm_bank()
                        proj(bu, uv, un_, lambda c: hT[:, c, :], hT_res, 128 * t)
                        u = ub[j % 2]
                        un = f"ub{j % 2}"
                        a = acc[j % 2]
                        an = f"acc{j % 2}"
                        P.op("pool", lambda e, u=u, j=j: e.tensor_copy(out=u[:, 0:2], in_=gcar[:, j, :]), ["gcar"], [un])
                        P.op("act", lambda e, u=u, bg=bg: e.activation(out=u[:, 2:TW + 2], in_=ps[bg][:], func=AF.Copy),
                             [f"ps{bg}"], [un])
                        P.op("pool", lambda e, u=u, j=j: e.tensor_copy(out=gcar[:, j, :], in_=u[:, TW:TW + 2]), [un], ["gcar"])
                        P.op("act", lambda e, a=a, j=j, bg=bg: e.activation(out=a[:], in_=ps[bg][:], func=AF.Identity,
                                                                            bias=pcol(P_FCB + j), scale=pcol(P_FCW + 44 + j)),
                             [f"ps{bg}", "prm"], [an])
                        P.op("dve", lambda e, u=u, a=a, j=j: e.scalar_tensor_tensor(out=a[:], in0=u[:, 1:TW + 1],
                                                                                    scalar=pcol(P_FCW + 22 + j), in1=a[:],
                                                                                    op0=ALU.mult, op1=ALU.add),
                             [un, an, "prm"], [an])
                        P.op("dve", lambda e, u=u, a=a, j=j: e.scalar_tensor_tensor(out=a[:], in0=u[:, 0:TW],
                                                                                    scalar=pcol(P_FCW + j), in1=a[:],
                                                                                    op0=ALU.mult, op1=ALU.add),
                             [un, an, "prm"], [an])
                        P.op("act", lambda e, a=a: e.activation(out=a[:], in_=a[:], func=AF.Silu), [an], [an])
                        P.op("dve", lambda e, a=a, jl=jl, bu=bu: e.tensor_tensor(out=gT[:, jl, :], in0=a[:], in1=ps[bu][:], op=ALU.mult),
                             [an, f"ps{bu}"], [f"gT{jl}"])
                if halo:
                    continue
                for o2 in range(4):
                    dv, dn = load_slab(f"wd{q * 4 + o2}", s_wd[q * 4 + o2], [128, 11, 256])
                    for o in range(2):
                        oc = 2 * o2 + o
                        bank = mm_bank()
                        proj(bank, dv, dn, lambda c: gT[:, c, :], lambda c: [f"gT{c}"], 128 * o, nk=11)
                        P.op("dve", lambda e, oc=oc, bank=bank: e.tensor_tensor(out=xT[:, oc, :], in0=xT[:, oc, :], in1=ps[bank][:],
                                                                                op=ALU.add),
                             [f"xT{oc}", f"ps{bank}"], [f"xT{oc}"])

        def ple(k):
            c0 = TW * (k - HALO_TILE - 1)
            cnt["p"] += 16
            P.dma("sp", pTs[:], s_pbf[:, :, c0:c0 + TW], ["pbf"], ["pT"], s_p, "s_p", cnt["p"])
            rmsnorm_x(P_GPLE)
            ppv, ppn = load_slab("wpp", s_wpp[:, :, :], [128, 2, 1024])
            for s in range(2):
                wv, wn = load_slab(f"wpg{s}", s_wpg[s], [128, 8, 512])
                for o in range(4):
                    oc = 4 * s + o
                    ba, bp = mm_bank(), mm_bank()
                    proj(ba, wv, wn, lambda c: hT[:, c, :], hT_res, 128 * o)
                    proj(bp, ppv, ppn, lambda c: pTs[:, c, :], lambda c: ["pT"], 128 * oc, nk=2)
                    sg = sig[oc % 2]
                    sn = f"sig{oc % 2}"
                    P.op("act", lambda e, sg=sg, ba=ba: e.activation(out=sg[:], in_=ps[ba][:], func=AF.Sigmoid), [f"ps{ba}"], [sn])
                    P.op("dve", lambda e, sg=sg, bp=bp: e.tensor_tensor(out=sg[:], in0=sg[:], in1=ps[bp][:], op=ALU.mult),
                         [sn, f"ps{bp}"], [sn])
                    P.op("dve", lambda e, sg=sg, oc=oc: e.tensor_tensor(out=xT[:, oc, :], in0=xT[:, oc, :], in1=sg[:], op=ALU.add),
                         [sn, f"xT{oc}"], [f"xT{oc}"])
            cnt["o"] += 16
            P.dma("sp", outT[:, :, c0:c0 + TW], xT[:], [f"xT{c}" for c in range(8)], [], s_o, "s_o", cnt["o"])

        for k in range(FIRST_TILE, last_tile + 1):
            load_x(k)
            rmsnorm_x(P_GMIX)
            if k < HALO_TILE:
                k_and_v(k)
                continue
            mixer_proj(k)
            k_and_v(k)
            attention(k)
            norm_y(4, P_GOA, C_O512)
            out_proj()
            rmsnorm_x(P_GFFN)
            ffn(halo=(k == HALO_TILE))
            if k > HALO_TILE:
                ple(k)
        P.wait_final("sp", s_o, cnt["o"])

        with nc.Block() as block:
            @block.sync
            def _(e):
                for f in P.q["sp"]:
                    f(e)

            @block.tensor
            def _(e):
                for f in P.q["pe"]:
                    f(e)

            @block.scalar
            def _(e):
                for f in P.q["act"]:
                    f(e)

            @block.vector
            def _(e):
                for f in P.q["dve"]:
                    f(e)

            @block.gpsimd
            def _(e):
                for f in P.q["pool"]:
                    f(e)
    return nc


def _const_table():
    cst = np.zeros((128, N_CBLK * 128), np.float32)
    i = np.arange(128)
    cst[:, C_ID * 128:(C_ID + 1) * 128] = np.eye(128, dtype=np.float32)
    cst[:, C_BD * 128:(C_BD + 1) * 128] = ((i[:, None] // 64) == (i[None, :] // 64)).astype(np.float32) / 64.0
    cst[:, C_O1024 * 128:(C_O1024 + 1) * 128] = 1.0 / 1024.0
    cst[:, C_O512 * 128:(C_O512 + 1) * 128] = 1.0 / 512.0
    a = i[:, None].astype(np.float64)
    j = i[None, :].astype(np.float64)
    for ci in range(12):
        c = 2.0 ** (ci - 5)
        ba = np.where(a >= j, -c * (j + 128 - a), NEGV)
        bb = np.where(a <= j, -c * (j - a), NEGV)
        cst[:, (C_MASK0 + 2 * ci) * 128:(C_MASK0 + 2 * ci + 1) * 128] = ba
        cst[:, (C_MASK0 + 2 * ci + 1) * 128:(C_MASK0 + 2 * ci + 2) * 128] = bb
    return cst


def _pack_params(inp):
    prm = np.zeros((128, NPRM), np.float32)

    def fm(v):
        v = np.asarray(v, np.float32).reshape(-1, 128)
        return v.T

    prm[:, P_GMIX:P_GMIX + 8] = fm(inp["g_mix"][0])
    for t in range(3):
        prm[:, P_CW + 4 * t:P_CW + 4 * t + 4] = fm(inp["conv_w"][0, t])
        prm[:, P_FCW + 22 * t:P_FCW + 22 * t + 22] = fm(inp["ffn_conv_w"][0, t])
    prm[:, P_CB:P_CB + 4] = fm(inp["conv_b"][0])
    prm[:, P_QG] = np.tile(np.asarray(inp["q_norm_g"][0], np.float32), 2)
    prm[:, P_KG] = np.tile(np.asarray(inp["k_norm_g"][0], np.float32), 2)
    prm[:, P_GOC:P_GOC + 4] = fm(inp["g_out_conv"][0])
    prm[:, P_GOA:P_GOA + 4] = fm(inp["g_out_attn"][0])
    prm[:, P_GFFN:P_GFFN + 8] = fm(inp["g_ffn"][0])
    prm[:, P_FCB:P_FCB + 22] = fm(inp["ffn_conv_b"][0])
    prm[:, P_GPLE:P_GPLE + 8] = fm(inp["g_ple"][0])
    return prm


_NC_CACHE = {}


def kernel(**inputs):
    inp = {k: np.asarray(v) for k, v in inputs.items()}
    x = inp["x"].astype(np.float32, copy=False)
    p = inp["p"].astype(np.float32, copy=False)
    if "nc" not in _NC_CACHE:
        _NC_CACHE["nc"] = build_program()
    nc = _NC_CACHE["nc"]
    cst = _const_table()
    prm = _pack_params(inp)
    shared = {
        "prm": prm, "cst": cst,
        "w_in": np.ascontiguousarray(inp["w_in"][0], np.float32),
        "w_out": np.ascontiguousarray(inp["w_out"][0], np.float32),
        "w_gate": np.ascontiguousarray(inp["w_gate"][0], np.float32),
        "w_up": np.ascontiguousarray(inp["w_up"][0], np.float32),
        "w_down": np.ascontiguousarray(inp["w_down"][0], np.float32),
        "w_pg": np.ascontiguousarray(inp["w_ple_gate"][0], np.float32),
        "w_pp": np.ascontiguousarray(inp["w_ple_proj"][0], np.float32),
    }
    in_maps = []
    for core in range(8):
        b, half = core // 2, core % 2
        s0 = half * OWN
        start = s0 - (NTOK - OWN)
        xT = np.zeros((D_MODEL, NTOK), np.float32)
        lo = max(start, 0)
        xT[:, lo - start:] = x[b, lo:s0 + OWN, :].T
        pos = start + np.arange(NTOK)
        vmask = (pos >= 0).astype(np.float32).reshape(NTOK // 128, 128).T
        pT = np.ascontiguousarray(p[0, b, s0:s0 + OWN, :].T)
        m = dict(shared)
        m.update({"xT": xT, "pT": pT, "vmask": np.ascontiguousarray(vmask)})
        in_maps.append(m)
    res = run_bass_kernel_spmd(nc, in_maps, core_ids=list(range(8)))
    out = np.empty((BATCH, SEQ, D_MODEL), np.float32)
    for core in range(8):
        b, half = core // 2, core % 2
        out[b, half * OWN:(half + 1) * OWN, :] = res.results[core]["outT"].T
    return out
```

```python
import math
from contextlib import ExitStack

import numpy as np
import concourse.bass as bass
import concourse.mybir as mybir
from concourse.bass_utils import run_bass_kernel_spmd

F32 = mybir.dt.float32
BF16 = mybir.dt.bfloat16
AF = mybir.ActivationFunctionType
ALU = mybir.AluOpType

D_MODEL = 1024
SEQ = 8192
BATCH = 4
D_FF = 2816
PLE_DIM = 256
EPS = 1e-6
TW = 512
OWN = 4096
FIRST_TILE = 1
N_KV = 4
HALO_TILE = 5
LAST_TILE = 13
NTOK = (LAST_TILE - FIRST_TILE + 1) * TW
NEGV = -240000.0

C_ID, C_BD, C_O1024, C_O512, C_MASK0 = 0, 1, 2, 3, 4
N_CBLK = 4 + 24
P_GMIX, P_CW, P_CB, P_QG, P_KG, P_GOC, P_GOA, P_GFFN, P_FCW, P_FCB, P_GPLE = 0, 8, 20, 24, 25, 26, 30, 34, 42, 108, 130
NPRM = 138

FF_SLABS = [(0, 4), (4, 4), (8, 3), (11, 4), (15, 4), (19, 3)]


class Prog:
    def __init__(self, esem):
        self.q = {k: [] for k in ("pe", "act", "dve", "pool", "sp")}
        self.cnt = {k: 0 for k in ("pe", "act", "dve", "pool")}
        self.esem = esem
        self.res = {}
        self.seen = {}

    def _wait(self, waiter, stamp, is_raw):
        sem, val, owner, sname = stamp
        if owner == waiter:
            if waiter == "pe" or not is_raw:
                return
        key = (waiter, sname)
        if self.seen.get(key, 0) >= val:
            return
        self.seen[key] = val
        self.q[waiter].append(lambda e, sem=sem, val=val: e.wait_ge(sem, val))

    def _deps(self, waiter, reads, writes):
        for r in reads:
            w = self.res.get(r)
            if w and w[0]:
                self._wait(waiter, w[0], True)
        for wn in writes:
            w = self.res.get(wn)
            if w:
                if w[0]:
                    self._wait(waiter, w[0], False)
                for st in w[1].values():
                    self._wait(waiter, st, False)

    def _update(self, key, stamp, reads, writes):
        for wn in writes:
            self.res[wn] = [stamp, {}]
        for r in reads:
            self.res.setdefault(r, [None, {}])[1][key] = stamp

    def op(self, eng, fn, reads=(), writes=(), inc=True):
        self._deps(eng, reads, writes)
        sem = self.esem[eng]
        if inc:
            self.cnt[eng] += 1
            stamp = (sem, self.cnt[eng], eng, eng)
            self.q[eng].append(lambda e, fn=fn, sem=sem: fn(e).then_inc(sem, 1))
        else:
            assert eng == "pe"
            stamp = (sem, self.cnt[eng] + 1, eng, eng)
            self.q[eng].append(lambda e, fn=fn: fn(e))
        self._update(eng, stamp, reads, writes)

    def dma(self, queue, out, in_, reads, writes, sem, sname, target):
        self._deps(queue, reads, writes)
        self.q[queue].append(lambda e, out=out, in_=in_, sem=sem: e.dma_start(out=out, in_=in_).then_inc(sem, 16))
        stamp = (sem, target, None, sname)
        self._update(sname, stamp, reads, writes)

    def dma_batch(self, queue, pairs, reads, writes, sem, sname, target):
        self._deps(queue, reads, writes)
        for out, in_ in pairs:
            self.q[queue].append(lambda e, out=out, in_=in_, sem=sem: e.dma_start(out=out, in_=in_).then_inc(sem, 16))
        stamp = (sem, target, None, sname)
        self._update(sname, stamp, reads, writes)

    def wait_final(self, waiter, sem, val):
        self.q[waiter].append(lambda e: e.wait_ge(sem, val))


def build_program(last_tile=LAST_TILE):
    nc = bass.Bass("TRN2", target_bir_lowering=False)
    dt = nc.dram_tensor
    xin = dt("xT", [D_MODEL, NTOK], F32, kind="ExternalInput").ap().rearrange("(c p) t -> p c t", p=128)
    pin = dt("pT", [PLE_DIM, OWN], F32, kind="ExternalInput").ap().rearrange("(c p) t -> p c t", p=128)
    vmin = dt("vmask", [128, NTOK // 128], F32, kind="ExternalInput").ap()
    prmin = dt("prm", [128, NPRM], F32, kind="ExternalInput").ap()
    cstin = dt("cst", [128, N_CBLK * 128], F32, kind="ExternalInput").ap()
    w_in = dt("w_in", [D_MODEL, 3072], F32, kind="ExternalInput").ap().rearrange("(c p) n -> p c n", p=128)
    w_out = dt("w_out", [D_MODEL, D_MODEL], F32, kind="ExternalInput").ap().rearrange("(c p) n -> p c n", p=128)
    w_gate = dt("w_gate", [D_MODEL, D_FF], F32, kind="ExternalInput").ap().rearrange("(c p) n -> p c n", p=128)
    w_up = dt("w_up", [D_MODEL, D_FF], F32, kind="ExternalInput").ap().rearrange("(c p) n -> p c n", p=128)
    w_down = dt("w_down", [D_FF, D_MODEL], F32, kind="ExternalInput").ap().rearrange("(q j p) n -> q p j n", q=2, j=11, p=128)
    w_pg = dt("w_pg", [D_MODEL, D_MODEL], F32, kind="ExternalInput").ap().rearrange("(c p) n -> p c n", p=128)
    w_pp = dt("w_pp", [PLE_DIM, D_MODEL], F32, kind="ExternalInput").ap().rearrange("(c p) n -> p c n", p=128)
    outT = dt("outT", [D_MODEL, OWN], F32, kind="ExternalOutput").ap().rearrange("(c p) t -> p c t", p=128)
    s_win = dt("s_win", [6, 128, 8, 512], BF16, kind="Internal").ap()
    s_wout = dt("s_wout", [2, 128, 8, 512], BF16, kind="Internal").ap()
    s_wg = dt("s_wg", [6, 128, 8, 512], BF16, kind="Internal").ap()
    s_wu = dt("s_wu", [6, 128, 8, 512], BF16, kind="Internal").ap()
    s_wd = dt("s_wd", [8, 128, 11, 256], BF16, kind="Internal").ap()
    s_wpg = dt("s_wpg", [2, 128, 8, 512], BF16, kind="Internal").ap()
    s_wpp = dt("s_wpp", [128, 2, 1024], BF16, kind="Internal").ap()
    s_pbf = dt("s_pbf", [128, 2, OWN], BF16, kind="Internal").ap()

    es = ExitStack()
    with es:
        def sb(name, shape, dtype):
            return es.enter_context(nc.sbuf_tensor(name, shape, dtype))

        def sem(name):
            return es.enter_context(nc.semaphore(name))

        KT = sb("KT", [128, 4, 4096], BF16)
        V1 = sb("V1", [128, 8, 520], BF16)
        V4 = sb("V4", [128, 8, 520], BF16)
        V16 = sb("V16", [128, 32, 520], BF16)
        XB = [sb(f"XB{i}", [128, 8, TW], F32) for i in range(2)]
        XB16 = [XB[i][:].bitcast(BF16) for i in range(2)]
        cur = [0]

        def X(c):
            return XB[cur[0]][:, c, :]

        def Xn(c):
            return f"X{cur[0]}_{c}"

        def YT(c):
            return XB16[1 - cur[0]][:, c // 2, (c % 2) * TW:(c % 2) * TW + TW]

        def YTn(c):
            return f"X{1 - cur[0]}_{c // 2}"

        def YF(j, lo=0, hi=128):
            return XB[1 - cur[0]][lo:hi, 4 + j, :]

        def YFn(j):
            return f"X{1 - cur[0]}_{4 + j}"

        hT = sb("hT", [128, 8, TW], BF16)
        QT = sb("QT", [128, 4, TW], BF16)
        gT = sb("gT", [128, 11, TW], BF16)
        wr = sb("wring", [128, 3, 4096], BF16)
        cbf = sb("cbf", [128, N_CBLK * 128], BF16)
        prm = sb("prms", [128, NPRM], F32)
        vm = sb("vms", [128, NTOK // 128], F32)
        onesf = sb("onesf", [128, 64], F32)
        sqb = [sb(f"sqb{i}", [128, TW], BF16) for i in range(2)]
        rstd = sb("rstd", [128, TW], F32)
        rq = [sb(f"rq{i}", [128, TW], F32) for i in range(2)]
        ub = [sb(f"ub{i}", [128, TW + 2], F32) for i in range(2)]
        acc = [sb(f"acc{i}", [128, TW], F32) for i in range(2)]
        xj = sb("xj", [128, TW], F32)
        osb = [sb(f"osb{i}", [128, TW], F32) for i in range(2)]
        rden = sb("rden", [128, TW], F32)
        PT = [sb(f"PT{i}", [128, TW], BF16) for i in range(3)]
        sig = [sb(f"sig{i}", [128, TW], F32) for i in range(2)]
        pTs = sb("pTs", [128, 2, TW], BF16)
        ucar = sb("ucar", [128, 4, 2], F32)
        gcar = sb("gcar", [128, 22, 2], F32)
        ps = [es.enter_context(nc.psum_tensor(f"ps{i}", [128, 512], F32)) for i in range(8)]

        esem = {k: sem("e_" + k) for k in ("pe", "act", "dve", "pool")}
        P = Prog(esem)
        s_p, s_o, s_c = sem("s_p"), sem("s_o"), sem("s_c")
        s_xb = [sem("s_x0"), sem("s_x1")]
        s_w = [sem(f"s_w{i}") for i in range(3)]
        s_vs = [sem(f"s_vs{i}") for i in range(2)]
        cnt = {"x": [0, 0], "p": 0, "o": 0, "c": 0, "w": [0, 0, 0], "vs": [0, 0]}

        def cblk(i, n=1):
            return cbf[:, i * 128:(i + n) * 128]

        def pcol(i):
            return prm[:, i:i + 1]

        s_cb = sem("s_cb")
        P.dma("pool", cbf[:], cstin[:, :], [], ["cbf"], s_cb, "s_cb", 16)
        P.dma_batch("sp", [(prm[:], prmin[:, :]), (vm[:], vmin[:, :])], [], ["prm", "vm"], s_c, "s_c", 32)
        P.op("dve", lambda e: e.memset(onesf[:], 1.0), [], ["onesf"])
        for c in range(4):
            P.op("pool", lambda e, c=c: e.memset(KT[:, c, :], 0.0), [], [f"KT{c}_{b}" for b in range(8)])
        P.op("pool", lambda e: e.memset(V1[:], 0.0), [], [f"V1_{i}" for i in range(8)])
        P.op("pool", lambda e: e.memset(V4[:], 0.0), [], [f"V4_{i}" for i in range(8)])
        P.op("pool", lambda e: e.memset(V16[:], 0.0), [], [f"V16_{i}" for i in range(32)])
        P.op("pool", lambda e: e.memset(ucar[:], 0.0), [], ["ucar"])
        P.op("pool", lambda e: e.memset(gcar[:], 0.0), [], ["gcar"])

        def cast_group(name, dst, src):
            s = sem("c_" + name)
            P.dma("pool", dst, src, [], [name], s, name, 16)

        cast_group("win4", s_win[4], w_in[:, :, 2048:2560])
        cast_group("win5", s_win[5], w_in[:, :, 2560:3072])
        for j in range(4):
            for t in range(4):
                cast_group(f"win{j}_{t}", s_win[j, :, :, 128 * t:128 * t + 128],
                           w_in[:, :, 512 * t + 128 * j:512 * t + 128 * j + 128])
        for s in range(2):
            cast_group(f"wout{s}", s_wout[s], w_out[:, :, 512 * s:512 * s + 512])
        for i, (j0, n) in enumerate(FF_SLABS):
            cast_group(f"wg{i}", s_wg[i, :, :, 0:128 * n], w_gate[:, :, 128 * j0:128 * (j0 + n)])
            cast_group(f"wu{i}", s_wu[i, :, :, 0:128 * n], w_up[:, :, 128 * j0:128 * (j0 + n)])
            if i % 3 == 2:
                q = i // 3
                for o2 in range(4):
                    cast_group(f"wd{q * 4 + o2}", s_wd[q * 4 + o2], w_down[q][:, :, 256 * o2:256 * o2 + 256])
        cast_group("wpp", s_wpp[:, :, :], w_pp[:, :, :])
        for s in range(2):
            cast_group(f"wpg{s}", s_wpg[s], w_pg[:, :, 512 * s:512 * s + 512])
        cast_group("pbf", s_pbf[:, :, :], pin[:, :, :])

        wstate = {"i": 0}

        def load_slab(scr_name, src_ap, shape):
            slot = wstate["i"] % 3
            wstate["i"] += 1
            n = 1
            for d in shape[1:]:
                n *= d
            dst = wr[:, slot, 0:n]
            if len(shape) == 3:
                dst = dst.rearrange("p (a b) -> p a b", b=shape[2])
            cnt["w"][slot] += 16
            P.dma("sp", dst, src_ap, [scr_name] if isinstance(scr_name, str) else list(scr_name), [f"w{slot}"], s_w[slot],
                  f"s_w{slot}", cnt["w"][slot])
            return dst, f"w{slot}"

        mmi = {"i": 0}

        mm_pool = [[0, 1, 2, 3, 4, 5, 6, 7]]

        def mm_bank():
            pool = mm_pool[0]
            b = pool[mmi["i"] % len(pool)]
            mmi["i"] += 1
            return b

        def proj(bank, wv, wname, rhs_fn, rhs_res, ocol, nk=8, extra_reads=()):
            for c in range(nk):
                P.op("pe", lambda e, c=c, rhs=rhs_fn(c): e.matmul(ps[bank][:], wv[:, c, ocol:ocol + 128], rhs,
                                                   start=(c == 0), stop=(c == nk - 1)),
                     [wname] + rhs_res(c) + list(extra_reads), [f"ps{bank}"], inc=(c == nk - 1))

        def sqrt_recip(dst, dname, bank):
            P.op("act", lambda e: e.activation(out=dst[:], in_=ps[bank][:], func=AF.Sqrt, bias=EPS, scale=1.0),
                 [f"ps{bank}"], [dname])
            P.op("dve", lambda e: e.reciprocal(out=dst[:], in_=dst[:]), [dname], [dname])

        def rmsnorm_x(gcol0):
            bank = mm_bank()
            for c in range(8):
                sq = sqb[c % 2]
                P.op("act", lambda e, xc=X(c), sq=sq: e.activation(out=sq[:], in_=xc, func=AF.Square),
                     [Xn(c)], [f"sqb{c % 2}"])
                P.op("pe", lambda e, c=c, sq=sq: e.matmul(ps[bank][:], cblk(C_O1024), sq[:], start=(c == 0), stop=(c == 7)),
                     [f"sqb{c % 2}", "cbf"], [f"ps{bank}"], inc=True)
            sqrt_recip(rstd, "rstd", bank)
            for c in range(8):
                P.op("dve", lambda e, c=c, xc=X(c): e.scalar_tensor_tensor(out=hT[:, c, :], in0=xc, scalar=pcol(gcol0 + c),
                                                                  in1=rstd[:], op0=ALU.mult, op1=ALU.mult),
                     [Xn(c), "rstd", "prm"], [f"hT{c}"])

        qki = {"i": 0}

        def qk_norm(bank, dst_ap, dst_res, gcol):
            i = qki["i"] % 2
            qki["i"] += 1
            sq = sqb[i]
            P.op("act", lambda e: e.activation(out=sq[:], in_=ps[bank][:], func=AF.Square), [f"ps{bank}"], [f"sqb{i}"])
            nb = mm_bank()
            P.op("pe", lambda e: e.matmul(ps[nb][:], cblk(C_BD), sq[:], start=True, stop=True), [f"sqb{i}", "cbf"], [f"ps{nb}"])
            sqrt_recip(rq[i], f"rq{i}", nb)
            P.op("dve", lambda e: e.scalar_tensor_tensor(out=dst_ap, in0=ps[bank][:], scalar=pcol(gcol), in1=rq[i][:],
                                                         op0=ALU.mult, op1=ALU.mult),
                 [f"ps{bank}", f"rq{i}", "prm"], dst_res)

        hT_res = lambda c: [f"hT{c}"]

        def kt_res(ch, k):
            return [f"KT{ch}_{k % 8}"]

        kvw = {}

        def k_and_v(k):
            u0 = TW * k
            rc = u0 % 4096
            reuse = FIRST_TILE < k < HALO_TILE
            if reuse:
                wv, wn = kvw["k"]
            else:
                wv, wn = load_slab("win4", s_win[4], [128, 8, 512])
                kvw["k"] = (wv, wn)
            for ch in range(4):
                bank = mm_bank()
                proj(bank, wv, wn, lambda c: hT[:, c, :], hT_res, 128 * ch)
                qk_norm(bank, KT[:, ch, rc:rc + TW], kt_res(ch, k), P_KG)
            if reuse:
                wv, wn = kvw["v"]
            else:
                wv, wn = load_slab("win5", s_win[5], [128, 8, 512])
                kvw["v"] = (wv, wn)
            T4, T16, kq = k, k // 4, k % 4
            par = k % 2
            pairs = []
            for n in range(4):
                tn = 4 * k + n
                slot = tn % 8
                bank = mm_bank()
                for c in range(8):
                    P.op("pe", lambda e, c=c, n=n, bank=bank, wv=wv: e.matmul(ps[bank][:], hT[:, c, 128 * n:128 * n + 128], wv[:, c, :],
                                                            start=(c == 0), stop=(c == 7)),
                         [wn, f"hT{c}"], [f"ps{bank}"], inc=(c == 7))
                v1v = V1[:, slot, :].rearrange("p (h e) -> p h e", e=65)
                P.op("act", lambda e, v1v=v1v, bank=bank: e.activation(out=v1v[:, :, 0:64],
                                                                       in_=ps[bank][:].rearrange("p (h d) -> p h d", d=64),
                                                                       func=AF.Copy),
                     [f"ps{bank}"], [f"V1_{slot}"])
                vcol = tn - 4 * FIRST_TILE
                P.op("pool", lambda e, v1v=v1v, vcol=vcol: e.tensor_copy(out=v1v[:, :, 64:65],
                                                                         in_=vm[:, vcol:vcol + 1].unsqueeze(1).broadcast_to([128, 8, 1])),
                     ["vm"], [f"V1_{slot}"])
                for r in range(4):
                    pairs.append((V4[32 * n:32 * n + 32, r * 2 + T4 % 2, :], V1[r:128:4, slot, :]))
                for r in range(16):
                    p0 = 32 * kq + 8 * n
                    pairs.append((V16[p0:p0 + 8, r * 2 + T16 % 2, :], V1[r:128:16, slot, :]))

            def scatter():
                cnt["vs"][par] += 16 * len(pairs)
                P.dma_batch("sp", pairs, [f"V1_{(4 * k + n) % 8}" for n in range(4)],
                            [f"V4_{r * 2 + T4 % 2}" for r in range(4)] + [f"V16_{r * 2 + T16 % 2}" for r in range(16)],
                            s_vs[par], f"s_vs{par}", cnt["vs"][par])
            return scatter

        def load_x(k):
            c0 = TW * (k - FIRST_TILE)
            b = k % 2
            cnt["x"][b] += 16
            P.dma("sp", XB[b][:], xin[:, :, c0:c0 + TW], [], [f"X{b}_{c}" for c in range(8)], s_xb[b], f"s_x{b}", cnt["x"][b])

        sti = {"i": 0}
        pti = {"i": 0}

        def attention(k):
            u0 = TW * k
            T16, kq = k // 4, k % 4
            o16 = 32 * kq
            units = []
            for h in range(8):
                ch, pb = h // 2, 64 * (h % 2)
                ob = 6 + h % 2
                branches = []
                for d in (1, 4, 16):
                    ci = 2 - h + int(math.log2(d)) + 5
                    mbase = (C_MASK0 + 2 * ci) * 128
                    for half in range(2):
                        blocks = []
                        if d == 1:
                            mrhs = cbf[:, mbase:mbase + 256].unsqueeze(1).broadcast_to([128, 2, 256])
                            for qb in (2 * half, 2 * half + 1):
                                T1 = 4 * k + qb
                                ents = []
                                for T in (T1 - 1, T1):
                                    col = (128 * T) % 4096
                                    ents.append((KT[pb:pb + 64, ch, col:col + 128], kt_res(ch, T // 4),
                                                 V1[:, T % 8, 65 * h:65 * h + 65], f"V1_{T % 8}"))
                                blocks.append((QT[pb:pb + 64, ch, 128 * qb:128 * qb + 128],
                                               ps[ob][0:65, 128 * qb:128 * qb + 128], ents))
                            bw = 128
                        elif d == 4:
                            mrhs = cbf[:, mbase:mbase + 256].unsqueeze(1).broadcast_to([128, 2, 256])
                            for r in (2 * half, 2 * half + 1):
                                ents = []
                                for T in (k - 1, k):
                                    base = (TW * T) % 4096
                                    ents.append((KT[pb:pb + 64, ch, base + r:base + 512:4], kt_res(ch, T),
                                                 V4[:, r * 2 + T % 2, 65 * h:65 * h + 65], f"V4_{r * 2 + T % 2}"))
                                blocks.append((QT[pb:pb + 64, ch, r:512:4], ps[ob][0:65, r:512:4], ents))
                            bw = 128
                        else:
                            mrhs = cbf[:, mbase:mbase + 256].rearrange("p (t j) -> p t j", j=128)[:, :, o16:o16 + 32] \
                                .unsqueeze(1).broadcast_to([128, 8, 2, 32])
                            for r in range(8 * half, 8 * half + 8):
                                ents = []
                                for T in (T16 - 1, T16):
                                    base = (2048 * T) % 4096
                                    kres = [f"KT{ch}_{(4 * T + i) % 8}" for i in range(4)]
                                    ents.append((KT[pb:pb + 64, ch, base + r:base + 2048:16], kres,
                                                 V16[:, r * 2 + T % 2, 65 * h:65 * h + 65], f"V16_{r * 2 + T % 2}"))
                                blocks.append((QT[pb:pb + 64, ch, r:512:16], ps[ob][0:65, r:512:16], ents))
                            bw = 32
                        branches.append((mrhs, blocks, bw))
                for gi, (mrhs, blocks, bw) in enumerate(branches):
                    units.append((h, mrhs, blocks, bw, gi == 0, gi == len(branches) - 1))

            def emit_S(u):
                h, mrhs, blocks, bw, first, last = u
                ch = h // 2
                sbk = 4 + sti["i"] % 2
                sti["i"] += 1
                P.op("pe", lambda e, mrhs=mrhs, sbk=sbk: e.matmul(ps[sbk][:], cblk(C_ID), mrhs, start=True, stop=False,
                                                                  skip_group_check=True),
                     ["cbf"], [f"ps{sbk}"], inc=False)
                nmm = 2 * len(blocks)
                i = 0
                for bi, (qap, oap, ents) in enumerate(blocks):
                    for ei, (kap, kres, vap, vres) in enumerate(ents):
                        c0 = (2 * bi + ei) * bw
                        i += 1
                        P.op("pe", lambda e, kap=kap, qap=qap, c0=c0, sbk=sbk, bw=bw, last=(i == nmm):
                             e.matmul(ps[sbk][:, c0:c0 + bw], kap, qap, start=False, stop=last, skip_group_check=True),
                             kres + [f"QT{ch}"], [f"ps{sbk}"], inc=(i == nmm))
                pt_i = pti["i"] % 3
                pti["i"] += 1
                pt = PT[pt_i]
                P.op("act", lambda e, pt=pt, sbk=sbk: e.activation(out=pt[:], in_=ps[sbk][:], func=AF.Exp, scale=0.125),
                     [f"ps{sbk}"], [f"PT{pt_i}"])
                return pt_i

            def emit_PV(u, pt_i):
                h, mrhs, blocks, bw, first, last = u
                ob = 6 + h % 2
                pt = PT[pt_i]
                nmm = 2 * len(blocks)
                i = 0
                for bi, (qap, oap, ents) in enumerate(blocks):
                    for ei, (kap, kres, vap, vres) in enumerate(ents):
                        c0 = (2 * bi + ei) * bw
                        i += 1
                        st = first and i == 1
                        P.op("pe", lambda e, oap=oap, vap=vap, pt=pt, c0=c0, bw=bw, st=st:
                             e.matmul(oap, vap, pt[:, c0:c0 + bw], start=st, stop=False, skip_group_check=True),
                             [vres, f"PT{pt_i}"], [f"ps{ob}"], inc=(i == nmm))

            def evac1(h):
                ob = 6 + h % 2
                ov = osb[h % 2]
                P.op("act", lambda e, ov=ov, ob=ob: e.activation(out=ov[0:65, :], in_=ps[ob][0:65, :], func=AF.Copy),
                     [f"ps{ob}"], [f"osb{h % 2}"])
                P.op("dve", lambda e, ov=ov: e.tensor_scalar_add(out=rden[64:65, :], in0=ov[64:65, :], scalar1=1e-30),
                     [f"osb{h % 2}"], ["rden"])
                P.op("dve", lambda e: e.reciprocal(out=rden[64:65, :], in_=rden[64:65, :]), ["rden"], ["rden"])

            def evac2(h):
                ch, pb = h // 2, 64 * (h % 2)
                ov = osb[h % 2]
                bb = mm_bank()
                P.op("pe", lambda e, bb=bb: e.matmul(ps[bb][0:64, :], onesf[64:65, 0:64], rden[64:65, :], start=True, stop=True),
                     ["rden", "onesf"], [f"ps{bb}"])
                P.op("dve", lambda e, ov=ov, bb=bb, yo=YF(ch, pb, pb + 64): e.tensor_tensor(out=yo, in0=ov[0:64, :],
                                                                                  in1=ps[bb][0:64, :], op=ALU.mult),
                     [f"osb{h % 2}", f"ps{bb}"], [YFn(ch)])

            prev = None
            pend2 = None
            for u in units:
                pt_i = emit_S(u)
                if pend2 is not None:
                    evac2(pend2)
                    pend2 = None
                if prev is not None:
                    emit_PV(*prev)
                    if prev[0][5]:
                        evac1(prev[0][0])
                        pend2 = prev[0][0]
                prev = (u, pt_i)
            emit_PV(*prev)
            evac1(prev[0][0])
            if pend2 is not None:
                evac2(pend2)
            evac2(prev[0][0])

        def norm_y(lo, gcol0, ones_blk):
            bank = mm_bank()
            for j in range(4):
                sq = sqb[j % 2]
                P.op("act", lambda e, yj=YF(j), sq=sq: e.activation(out=sq[:], in_=yj, func=AF.Square),
                     [YFn(j)], [f"sqb{j % 2}"])
                P.op("pe", lambda e, j=j, sq=sq: e.matmul(ps[bank][:], cblk(ones_blk), sq[:], start=(j == 0), stop=(j == 3)),
                     [f"sqb{j % 2}", "cbf"], [f"ps{bank}"], inc=True)
            sqrt_recip(rstd, "rstd", bank)
            for j in range(4):
                P.op("dve", lambda e, j=j, yo=YT(lo + j), yj=YF(j): e.scalar_tensor_tensor(out=yo, in0=yj, scalar=pcol(gcol0 + j),
                                                                  in1=rstd[:], op0=ALU.mult, op1=ALU.mult),
                     [YFn(j), "rstd", "prm"], [YTn(lo + j)])

        def mixer_proj(k, after_last_load=None):
            for j in range(4):
                wv, wn = load_slab([f"win{j}_{t}" for t in range(4)], s_win[j], [128, 8, 512])
                if j == 3 and after_last_load is not None:
                    after_last_load()
                bB, bC, bX, bQ = mm_bank(), mm_bank(), mm_bank(), mm_bank()
                for t, bank in enumerate((bB, bC, bX, bQ)):
                    proj(bank, wv, wn, lambda c: hT[:, c, :], hT_res, 128 * t)
                u = ub[j % 2]
                un = f"ub{j % 2}"
                a = acc[j % 2]
                an = f"acc{j % 2}"
                P.op("act", lambda e, bX=bX: e.activation(out=xj[:], in_=ps[bX][:], func=AF.Copy), [f"ps{bX}"], ["xj"])
                P.op("pool", lambda e, u=u, j=j: e.tensor_copy(out=u[:, 0:2], in_=ucar[:, j, :]), ["ucar"], [un])
                P.op("dve", lambda e, u=u, bC=bC: e.tensor_tensor(out=u[:, 2:TW + 2], in0=ps[bC][:], in1=xj[:], op=ALU.mult),
                     [f"ps{bC}", "xj"], [un])
                P.op("pool", lambda e, u=u, j=j: e.tensor_copy(out=ucar[:, j, :], in_=u[:, TW:TW + 2]), [un], ["ucar"])
                P.op("act", lambda e, u=u, a=a, j=j: e.activation(out=a[:], in_=u[:, 2:TW + 2], func=AF.Identity,
                                                                  bias=pcol(P_CB + j), scale=pcol(P_CW + 8 + j)),
                     [un, "prm"], [an])
                P.op("dve", lambda e, u=u, a=a, j=j: e.scalar_tensor_tensor(out=a[:], in0=u[:, 1:TW + 1], scalar=pcol(P_CW + 4 + j),
                                                                            in1=a[:], op0=ALU.mult, op1=ALU.add),
                     [un, an, "prm"], [an])
                P.op("dve", lambda e, u=u, a=a, j=j: e.scalar_tensor_tensor(out=a[:], in0=u[:, 0:TW], scalar=pcol(P_CW + j),
                                                                            in1=a[:], op0=ALU.mult, op1=ALU.add),
                     [un, an, "prm"], [an])
                P.op("dve", lambda e, a=a, yj=YF(j), bB=bB: e.tensor_tensor(out=yj, in0=ps[bB][:], in1=a[:], op=ALU.mult),
                     [f"ps{bB}", an], [YFn(j)])
                qk_norm(bQ, QT[:, j, :], [f"QT{j}"], P_QG)
            norm_y(0, P_GOC, C_O512)

        def out_proj():
            for s in range(2):
                wv, wn = load_slab(f"wout{s}", s_wout[s], [128, 8, 512])
                for o in range(4):
                    oc = 4 * s + o
                    bank = mm_bank()
                    proj(bank, wv, wn, YT, lambda c: [YTn(c)], 128 * o)
                    P.op("dve", lambda e, xo=X(oc), bank=bank: e.tensor_tensor(out=xo, in0=xo, in1=ps[bank][:], op=ALU.add),
                         [Xn(oc), f"ps{bank}"], [Xn(oc)])

        def ffn(halo, after_first_load=None):
            for q in range(2):
                for i3 in range(3):
                    i = 3 * q + i3
                    j0, n = FF_SLABS[i]
                    gv, gn = load_slab(f"wg{i}", s_wg[i, :, :, 0:128 * n], [128, 8, 128 * n])
                    if not halo:
                        uv, un_ = load_slab(f"wu{i}", s_wu[i, :, :, 0:128 * n], [128, 8, 128 * n])
                    if i == 0 and after_first_load is not None:
                        after_first_load()
                    for t in range(n):
                        j = j0 + t
                        jl = j - 11 * q
                        bg = mm_bank()
                        proj(bg, gv, gn, lambda c: hT[:, c, :], hT_res, 128 * t)
                        if halo:
                            P.op("act", lambda e, j=j, bg=bg: e.activation(out=gcar[:, j, :], in_=ps[bg][:, TW - 2:TW], func=AF.Copy),
                                 [f"ps{bg}"], ["gcar"])
                            continue
                        bu = mm_bank()
                        proj(bu, uv, un_, lambda c: hT[:, c, :], hT_res, 128 * t)
                        u = ub[j % 2]
                        un = f"ub{j % 2}"
                        a = acc[j % 2]
                        an = f"acc{j % 2}"
                        P.op("pool", lambda e, u=u, j=j: e.tensor_copy(out=u[:, 0:2], in_=gcar[:, j, :]), ["gcar"], [un])
                        P.op("act", lambda e, u=u, bg=bg: e.activation(out=u[:, 2:TW + 2], in_=ps[bg][:], func=AF.Copy),
                             [f"ps{bg}"], [un])
                        P.op("pool", lambda e, u=u, j=j: e.tensor_copy(out=gcar[:, j, :], in_=u[:, TW:TW + 2]), [un], ["gcar"])
                        P.op("act", lambda e, a=a, j=j, bg=bg: e.activation(out=a[:], in_=ps[bg][:], func=AF.Identity,
                                                                            bias=pcol(P_FCB + j), scale=pcol(P_FCW + 44 + j)),
                             [f"ps{bg}", "prm"], [an])
                        P.op("dve", lambda e, u=u, a=a, j=j: e.scalar_tensor_tensor(out=a[:], in0=u[:, 1:TW + 1],
                                                                                    scalar=pcol(P_FCW + 22 + j), in1=a[:],
                                                                                    op0=ALU.mult, op1=ALU.add),
                             [un, an, "prm"], [an])
                        P.op("dve", lambda e, u=u, a=a, j=j: e.scalar_tensor_tensor(out=a[:], in0=u[:, 0:TW],
                                                                                    scalar=pcol(P_FCW + j), in1=a[:],
                                                                                    op0=ALU.mult, op1=ALU.add),
                             [un, an, "prm"], [an])
                        P.op("act", lambda e, a=a: e.activation(out=a[:], in_=a[:], func=AF.Silu), [an], [an])
                        P.op("dve", lambda e, a=a, jl=jl, bu=bu: e.tensor_tensor(out=gT[:, jl, :], in0=a[:], in1=ps[bu][:], op=ALU.mult),
                             [an, f"ps{bu}"], [f"gT{jl}"])
                if halo:
                    continue
                for o2 in range(4):
                    dv, dn = load_slab(f"wd{q * 4 + o2}", s_wd[q * 4 + o2], [128, 11, 256])
                    for o in range(2):
                        oc = 2 * o2 + o
                        bank = mm_bank()
                        proj(bank, dv, dn, lambda c: gT[:, c, :], lambda c: [f"gT{c}"], 128 * o, nk=11)
                        P.op("dve", lambda e, xo=X(oc), bank=bank: e.tensor_tensor(out=xo, in0=xo, in1=ps[bank][:], op=ALU.add),
                             [Xn(oc), f"ps{bank}"], [Xn(oc)])

        def ple(k):
            c0 = TW * (k - HALO_TILE - 1)
            cnt["p"] += 16
            P.dma("sp", pTs[:], s_pbf[:, :, c0:c0 + TW], ["pbf"], ["pT"], s_p, "s_p", cnt["p"])
            rmsnorm_x(P_GPLE)
            ppv, ppn = load_slab("wpp", s_wpp[:, :, :], [128, 2, 1024])
            for s in range(2):
                wv, wn = load_slab(f"wpg{s}", s_wpg[s], [128, 8, 512])
                for o in range(4):
                    oc = 4 * s + o
                    ba, bp = mm_bank(), mm_bank()
                    proj(ba, wv, wn, lambda c: hT[:, c, :], hT_res, 128 * o)
                    proj(bp, ppv, ppn, lambda c: pTs[:, c, :], lambda c: ["pT"], 128 * oc, nk=2)
                    sg = sig[oc % 2]
                    sn = f"sig{oc % 2}"
                    P.op("act", lambda e, sg=sg, ba=ba: e.activation(out=sg[:], in_=ps[ba][:], func=AF.Sigmoid), [f"ps{ba}"], [sn])
                    P.op("dve", lambda e, sg=sg, bp=bp: e.tensor_tensor(out=sg[:], in0=sg[:], in1=ps[bp][:], op=ALU.mult),
                         [sn, f"ps{bp}"], [sn])
                    P.op("dve", lambda e, sg=sg, xo=X(oc): e.tensor_tensor(out=xo, in0=xo, in1=sg[:], op=ALU.add),
                         [sn, Xn(oc)], [Xn(oc)])
            cnt["o"] += 16
            P.dma("sp", outT[:, :, c0:c0 + TW], XB[cur[0]][:], [Xn(c) for c in range(8)], [], s_o, "s_o", cnt["o"])

        load_x(FIRST_TILE)
        for k in range(FIRST_TILE, last_tile + 1):
            cur[0] = k % 2
            nxt = (lambda k=k: load_x(k + 1)) if k < last_tile else None
            rmsnorm_x(P_GMIX)
            if k < HALO_TILE:
                if nxt:
                    nxt()
                k_and_v(k)()
                continue
            scat = k_and_v(k)
            mixer_proj(k, after_last_load=scat)
            mm_pool[0] = [0, 1, 2, 3]
            attention(k)
            mm_pool[0] = [0, 1, 2, 3, 4, 5, 6, 7]
            norm_y(4, P_GOA, C_O512)
            out_proj()
            rmsnorm_x(P_GFFN)
            ffn(halo=(k == HALO_TILE), after_first_load=nxt)
            if k > HALO_TILE:
                ple(k)
        P.wait_final("sp", s_o, cnt["o"])

        with nc.Block() as block:
            @block.sync
            def _(e):
                for f in P.q["sp"]:
                    f(e)

            @block.tensor
            def _(e):
                for f in P.q["pe"]:
                    f(e)

            @block.scalar
            def _(e):
                for f in P.q["act"]:
                    f(e)

            @block.vector
            def _(e):
                for f in P.q["dve"]:
                    f(e)

            @block.gpsimd
            def _(e):
                for f in P.q["pool"]:
                    f(e)
    return nc


def _const_table():
    cst = np.zeros((128, N_CBLK * 128), np.float32)
    i = np.arange(128)
    cst[:, C_ID * 128:(C_ID + 1) * 128] = np.eye(128, dtype=np.float32)
    cst[:, C_BD * 128:(C_BD + 1) * 128] = ((i[:, None] // 64) == (i[None, :] // 64)).astype(np.float32) / 64.0
    cst[:, C_O1024 * 128:(C_O1024 + 1) * 128] = 1.0 / 1024.0
    cst[:, C_O512 * 128:(C_O512 + 1) * 128] = 1.0 / 512.0
    a = i[:, None].astype(np.float64)
    j = i[None, :].astype(np.float64)
    for ci in range(12):
        c = 2.0 ** (ci - 5)
        ba = np.where(a >= j, -c * (j + 128 - a), NEGV)
        bb = np.where(a <= j, -c * (j - a), NEGV)
        cst[:, (C_MASK0 + 2 * ci) * 128:(C_MASK0 + 2 * ci + 1) * 128] = ba
        cst[:, (C_MASK0 + 2 * ci + 1) * 128:(C_MASK0 + 2 * ci + 2) * 128] = bb
    return cst


def _pack_params(inp):
    prm = np.zeros((128, NPRM), np.float32)

    def fm(v):
        v = np.asarray(v, np.float32).reshape(-1, 128)
        return v.T

    prm[:, P_GMIX:P_GMIX + 8] = fm(inp["g_mix"][0])
    for t in range(3):
        prm[:, P_CW + 4 * t:P_CW + 4 * t + 4] = fm(inp["conv_w"][0, t])
        prm[:, P_FCW + 22 * t:P_FCW + 22 * t + 22] = fm(inp["ffn_conv_w"][0, t])
    prm[:, P_CB:P_CB + 4] = fm(inp["conv_b"][0])
    prm[:, P_QG] = np.tile(np.asarray(inp["q_norm_g"][0], np.float32), 2)
    prm[:, P_KG] = np.tile(np.asarray(inp["k_norm_g"][0], np.float32), 2)
    prm[:, P_GOC:P_GOC + 4] = fm(inp["g_out_conv"][0])
    prm[:, P_GOA:P_GOA + 4] = fm(inp["g_out_attn"][0])
    prm[:, P_GFFN:P_GFFN + 8] = fm(inp["g_ffn"][0])
    prm[:, P_FCB:P_FCB + 22] = fm(inp["ffn_conv_b"][0])
    prm[:, P_GPLE:P_GPLE + 8] = fm(inp["g_ple"][0])
    return prm


_NC_CACHE = {}


def kernel(**inputs):
    inp = {k: np.asarray(v) for k, v in inputs.items()}
    x = inp["x"].astype(np.float32, copy=False)
    p = inp["p"].astype(np.float32, copy=False)
    if "nc" not in _NC_CACHE:
        _NC_CACHE["nc"] = build_program()
    nc = _NC_CACHE["nc"]
    cst = _const_table()
    prm = _pack_params(inp)
    shared = {
        "prm": prm, "cst": cst,
        "w_in": np.ascontiguousarray(inp["w_in"][0], np.float32),
        "w_out": np.ascontiguousarray(inp["w_out"][0], np.float32),
        "w_gate": np.ascontiguousarray(inp["w_gate"][0], np.float32),
        "w_up": np.ascontiguousarray(inp["w_up"][0], np.float32),
        "w_down": np.ascontiguousarray(inp["w_down"][0], np.float32),
        "w_pg": np.ascontiguousarray(inp["w_ple_gate"][0], np.float32),
        "w_pp": np.ascontiguousarray(inp["w_ple_proj"][0], np.float32),
    }
    in_maps = []
    for core in range(8):
        b, half = core // 2, core % 2
        s0 = half * OWN
        start = s0 - (NTOK - OWN)
        xT = np.zeros((D_MODEL, NTOK), np.float32)
        lo = max(start, 0)
        xT[:, lo - start:] = x[b, lo:s0 + OWN, :].T
        pos = start + np.arange(NTOK)
        vmask = (pos >= 0).astype(np.float32).reshape(NTOK // 128, 128).T
        pT = np.ascontiguousarray(p[0, b, s0:s0 + OWN, :].T)
        m = dict(shared)
        m.update({"xT": xT, "pT": pT, "vmask": np.ascontiguousarray(vmask)})
        in_maps.append(m)
    res = run_bass_kernel_spmd(nc, in_maps, core_ids=list(range(8)))
    out = np.empty((BATCH, SEQ, D_MODEL), np.float32)
    for core in range(8):
        b, half = core // 2, core % 2
        out[b, half * OWN:(half + 1) * OWN, :] = res.results[core]["outT"].T
    return out
```
